# Optimizing a Trainium2 kernel written in Bass

```python
import jax
import jax.numpy as jnp
from jax import lax
import numpy as np

D_MODEL = 1024
BATCH = 4
SEQ = 4096
DEPTH = 2

GRID_W = 64
CTX_LEN = 256
N_MIXERS = 2
NORM_EPS = 1e-6
FFN_HIDDEN = -(-8 * D_MODEL // (3 * 256)) * 256
LRU_WIDTH = D_MODEL
LRU_BLOCK = 256
LRU_BLOCKS = LRU_WIDTH // LRU_BLOCK
CONV_W = 4
CONV_LEFT = 1
RG_C = 8.0
MLA_HEADS = 8
Q_LORA = 384
KV_LORA = 256
QK_NOPE = 128
QK_ROPE = 64
V_HEAD = 128
SM_SCALE = (QK_NOPE + QK_ROPE) ** -0.5
Q_BLOCK = 128
ROPE_THETA = 10000.0

kernel_name = 'hybrid_rglru_mla_prefix_dit'


def _rmsnorm(x, g):
    xf = x.astype(jnp.float32)
    y = xf * lax.rsqrt(jnp.mean(xf * xf, axis=-1, keepdims=True) + NORM_EPS)
    return (y * g.astype(jnp.float32)).astype(x.dtype)


def _modulate(h, shift, scale):
    return h * (1.0 + scale) + shift


def _swiglu(h, w_in, w_out):
    g, u = jnp.split(h @ w_in, 2, axis=-1)
    return (jax.nn.silu(g) * u) @ w_out


def _axial_rope_tables(n_tokens, dtype):
    rows = n_tokens // GRID_W
    row_ids = jnp.repeat(jnp.arange(rows, dtype=jnp.float32), GRID_W)
    col_ids = jnp.tile(jnp.arange(GRID_W, dtype=jnp.float32), rows)
    axis_dim = QK_ROPE // 2
    inv_freq = 1.0 / (ROPE_THETA ** (jnp.arange(0, axis_dim, 2, dtype=jnp.float32) / axis_dim))
    ang_r = (row_ids[:, None] * inv_freq)[:, None, :]
    ang_c = (col_ids[:, None] * inv_freq)[:, None, :]
    return (jnp.cos(ang_r).astype(dtype), jnp.sin(ang_r).astype(dtype),
            jnp.cos(ang_c).astype(dtype), jnp.sin(ang_c).astype(dtype))


def _rotate_axis(x, cos, sin):
    x1, x2 = jnp.split(x, 2, axis=-1)
    return jnp.concatenate([x1 * cos - x2 * sin, x2 * cos + x1 * sin], axis=-1)


def _apply_rope_2d(x, tabs):
    cos_r, sin_r, cos_c, sin_c = tabs
    half = QK_ROPE // 2
    return jnp.concatenate([_rotate_axis(x[..., :half], cos_r, sin_r),
                            _rotate_axis(x[..., half:], cos_c, sin_c)], axis=-1)


def _dwconv_centred(x, w, b):
    t = x.shape[1]
    xp = jnp.pad(x, ((0, 0), (CONV_LEFT, CONV_W - 1 - CONV_LEFT), (0, 0)))
    y = b
    for k in range(CONV_W):
        y = y + xp[:, k:k + t] * w[k]
    return y


def _blockdiag(x, w):
    xb = x.reshape(x.shape[:-1] + (LRU_BLOCKS, LRU_BLOCK))
    return jnp.einsum('btnc,ncd->btnd', xb, w).reshape(x.shape)


def _rglru_coeffs(xc, gw, gb, lam):
    xf = xc.astype(jnp.float32)
    r = jax.nn.sigmoid(_blockdiag(xf, gw[0].astype(jnp.float32)) + gb[0].astype(jnp.float32))
    ig = jax.nn.sigmoid(_blockdiag(xf, gw[1].astype(jnp.float32)) + gb[1].astype(jnp.float32))
    log_a = -RG_C * r * jax.nn.softplus(-lam.astype(jnp.float32))
    a = jnp.exp(log_a)
    b = jnp.sqrt(-jnp.expm1(2.0 * log_a)) * (ig * xf)
    return a, b


def _combine(left, right):
    a1, b1 = left
    a2, b2 = right
    return a1 * a2, a2 * b1 + b2


def _linear_scan(a, b, h0, reverse):
    if reverse:
        a = jnp.flip(a, axis=1)
        b = jnp.flip(b, axis=1)
    b = b.at[:, 0].add(a[:, 0] * h0)
    _, h = lax.associative_scan(_combine, (a, b), axis=1)
    if reverse:
        h = jnp.flip(h, axis=1)
        return h, h[:, 0]
    return h, h[:, -1]


def _rglru_mixer(h_ctx, h_lat, w_in, conv_w, conv_b, gate_w, gate_b, lam, w_out, need_ctx):
    g_lat, u_lat = jnp.split(h_lat @ w_in, 2, axis=-1)
    g_ctx, u_ctx = jnp.split(h_ctx @ w_in, 2, axis=-1)
    u_lat = _dwconv_centred(u_lat, conv_w, conv_b)
    u_ctx = _dwconv_centred(u_ctx, conv_w, conv_b)
    h0 = jnp.zeros((h_lat.shape[0], LRU_WIDTH), jnp.float32)
    y_lat = []
    y_ctx = []
    for d, reverse in enumerate((False, True)):
        a_c, b_c = _rglru_coeffs(u_ctx, gate_w[d], gate_b[d], lam[d])
        hc, hc_final = _linear_scan(a_c, b_c, h0, reverse)
        a_l, b_l = _rglru_coeffs(u_lat, gate_w[d], gate_b[d], lam[d])
        hl, _ = _linear_scan(a_l, b_l, hc_final, reverse)
        y_lat.append(hl)
        y_ctx.append(hc)
    out_lat = ((y_lat[0] + y_lat[1]).astype(g_lat.dtype) * jax.nn.gelu(g_lat)) @ w_out
    out_ctx = None
    if need_ctx:
        out_ctx = ((y_ctx[0] + y_ctx[1]).astype(g_ctx.dtype) * jax.nn.gelu(g_ctx)) @ w_out
    return out_ctx, out_lat


def _attend(q, k, v):
    s = jnp.einsum('bqhd,bkhd->bhqk', q, k).astype(jnp.float32) * SM_SCALE
    p = jax.nn.softmax(s, axis=-1).astype(v.dtype)
    return jnp.einsum('bhqk,bkhd->bqhd', p, v)


def _blocked_attention(q, k, v):
    b, s, h, dq = q.shape
    nb = s // Q_BLOCK
    qb = jnp.moveaxis(q.reshape(b, nb, Q_BLOCK, h, dq), 1, 0)
    ob = lax.map(lambda qi: _attend(qi, k, v), qb)
    return jnp.moveaxis(ob, 0, 1).reshape(b, s, h, v.shape[-1])


def _mla_mixer(h_ctx, h_lat, w_in, q_norm_g, kv_norm_g, w_uq, w_ukv, w_o, tabs, need_ctx):
    n_ctx = h_ctx.shape[1]
    b, s, _ = h_lat.shape
    h_all = jnp.concatenate([h_ctx, h_lat], axis=1)
    t = h_all.shape[1]
    cq, ckv, kr = jnp.split(h_all @ w_in, [Q_LORA, Q_LORA + KV_LORA], axis=-1)
    q = (_rmsnorm(cq, q_norm_g) @ w_uq).reshape(b, t, MLA_HEADS, QK_NOPE + QK_ROPE)
    kv = (_rmsnorm(ckv, kv_norm_g) @ w_ukv).reshape(b, t, MLA_HEADS, QK_NOPE + V_HEAD)
    k_nope, v = jnp.split(kv, [QK_NOPE], axis=-1)
    kr = kr[:, :, None, :]
    k_rope = jnp.concatenate([kr[:, :n_ctx], _apply_rope_2d(kr[:, n_ctx:], tabs)], axis=1)
    k = jnp.concatenate([k_nope, jnp.broadcast_to(k_rope, (b, t, MLA_HEADS, QK_ROPE))], axis=-1)
    q_l = q[:, n_ctx:]
    q_lat = jnp.concatenate([q_l[..., :QK_NOPE], _apply_rope_2d(q_l[..., QK_NOPE:], tabs)], axis=-1)
    o_lat = _blocked_attention(q_lat, k, v)
    out_lat = o_lat.reshape(b, s, MLA_HEADS * V_HEAD) @ w_o
    out_ctx = None
    if need_ctx:
        o_ctx = _attend(q[:, :n_ctx], k[:, :n_ctx], v[:, :n_ctx])
        out_ctx = o_ctx.reshape(b, n_ctx, MLA_HEADS * V_HEAD) @ w_o
    return out_ctx, out_lat


def setup_inputs(seed: int = 0) -> dict:
    key = jax.random.key(seed)
    ks = jax.random.split(key, 24)
    f32 = jnp.float32
    n_lru = (DEPTH + 1) // 2
    n_mla = DEPTH // 2
    r = LRU_WIDTH

    def nrm(k, shape, fan_in):
        return jax.random.normal(k, shape, f32) * fan_in ** -0.5

    def gain(k, shape):
        return 1.0 + 0.05 * jax.random.normal(k, shape, f32)

    a0 = jax.random.uniform(ks[14], (n_lru, 2, r), f32, 0.9, 0.999)
    s = a0 ** (1.0 / RG_C)
    lam = jnp.log(s) - jnp.log1p(-s)
    return {
        'x': jax.random.normal(ks[0], (BATCH, SEQ, D_MODEL), f32),
        'c': jax.random.normal(ks[1], (BATCH, D_MODEL), f32),
        'ctx': jax.random.normal(ks[2], (BATCH, CTX_LEN, D_MODEL), f32),
        'c_ctx': jax.random.normal(ks[3], (D_MODEL,), f32),
        'ada_w': nrm(ks[4], (DEPTH, D_MODEL, 6 * D_MODEL), D_MODEL),
        'ada_b': 0.01 * jax.random.normal(ks[5], (DEPTH, 6 * D_MODEL), f32),
        'norm_mix_g': gain(ks[6], (DEPTH, D_MODEL)),
        'norm_ffn_g': gain(ks[7], (DEPTH, D_MODEL)),
        'ffn_w_in': nrm(ks[8], (DEPTH, D_MODEL, 2 * FFN_HIDDEN), D_MODEL),
        'ffn_w_out': nrm(ks[9], (DEPTH, FFN_HIDDEN, D_MODEL), FFN_HIDDEN),
        'lru_w_in': nrm(ks[10], (n_lru, D_MODEL, 2 * r), D_MODEL),
        'lru_conv_w': nrm(ks[11], (n_lru, CONV_W, r), CONV_W),
        'lru_conv_b': 0.01 * jax.random.normal(ks[12], (n_lru, r), f32),
        'lru_gate_w': nrm(ks[13], (n_lru, 2, 2, LRU_BLOCKS, LRU_BLOCK, LRU_BLOCK), LRU_BLOCK),
        'lru_gate_b': 0.01 * jax.random.normal(ks[15], (n_lru, 2, 2, r), f32),
        'lru_lambda': lam,
        'lru_w_out': nrm(ks[16], (n_lru, r, D_MODEL), r),
        'mla_w_in': nrm(ks[17], (n_mla, D_MODEL, Q_LORA + KV_LORA + QK_ROPE), D_MODEL),
        'mla_q_norm_g': gain(ks[18], (n_mla, Q_LORA)),
        'mla_kv_norm_g': gain(ks[19], (n_mla, KV_LORA)),
        'mla_w_uq': nrm(ks[20], (n_mla, Q_LORA, MLA_HEADS * (QK_NOPE + QK_ROPE)), Q_LORA),
        'mla_w_ukv': nrm(ks[21], (n_mla, KV_LORA, MLA_HEADS * (QK_NOPE + V_HEAD)), KV_LORA),
        'mla_w_o': nrm(ks[22], (n_mla, MLA_HEADS * V_HEAD, D_MODEL), MLA_HEADS * V_HEAD),
        'final_norm_g': gain(ks[23], (D_MODEL,)),
    }


def reference(x, c, ctx, c_ctx, ada_w, ada_b, norm_mix_g, norm_ffn_g, ffn_w_in, ffn_w_out,
              lru_w_in, lru_conv_w, lru_conv_b, lru_gate_w, lru_gate_b, lru_lambda, lru_w_out,
              mla_w_in, mla_q_norm_g, mla_kv_norm_g, mla_w_uq, mla_w_ukv, mla_w_o, final_norm_g):
    tabs = _axial_rope_tables(x.shape[1], x.dtype)
    silu_c = jax.nn.silu(c)
    silu_cc = jax.nn.silu(c_ctx)
    for i in range(DEPTH):
        need_ctx = i < DEPTH - 1
        j = i // N_MIXERS
        mod_lat = jnp.split((silu_c @ ada_w[i] + ada_b[i])[:, None, :], 6, axis=-1)
        mod_ctx = jnp.split(silu_cc @ ada_w[i] + ada_b[i], 6, axis=-1)
        h_lat = _modulate(_rmsnorm(x, norm_mix_g[i]), mod_lat[0], mod_lat[1])
        h_ctx = _modulate(_rmsnorm(ctx, norm_mix_g[i]), mod_ctx[0], mod_ctx[1])
        if i % N_MIXERS == 0:
            o_ctx, o_lat = _rglru_mixer(h_ctx, h_lat, lru_w_in[j], lru_conv_w[j], lru_conv_b[j],
                                        lru_gate_w[j], lru_gate_b[j], lru_lambda[j], lru_w_out[j],
                                        need_ctx)
        else:
            o_ctx, o_lat = _mla_mixer(h_ctx, h_lat, mla_w_in[j], mla_q_norm_g[j], mla_kv_norm_g[j],
                                      mla_w_uq[j], mla_w_ukv[j], mla_w_o[j], tabs, need_ctx)
        x = x + mod_lat[2] * o_lat
        x = x + mod_lat[5] * _swiglu(_modulate(_rmsnorm(x, norm_ffn_g[i]), mod_lat[3], mod_lat[4]),
                                     ffn_w_in[i], ffn_w_out[i])
        if need_ctx:
            ctx = ctx + mod_ctx[2] * o_ctx
            ctx = ctx + mod_ctx[5] * _swiglu(_modulate(_rmsnorm(ctx, norm_ffn_g[i]), mod_ctx[3], mod_ctx[4]),
                                             ffn_w_in[i], ffn_w_out[i])
    return _rmsnorm(x, final_norm_g)
```

```python
import numpy as np
from contextlib import ExitStack
import concourse.bass as bass
import concourse.mybir as mybir
from concourse.bass_utils import run_bass_kernel_spmd

F32 = mybir.dt.float32
BF16 = mybir.dt.bfloat16
AF = mybir.ActivationFunctionType
ALU = mybir.AluOpType

D = 1024
NCH = 8
SEQ = 4096
CTX = 256
HALF = 2048
TOK = CTX + HALF
FFN_H = 2816
NHC = 22
EPS = 1e-6
NHEAD = 8
SM_SCALE = 192.0 ** -0.5
NKEY = CTX + SEQ

BL = [(0, 256)] + [(256 + 512 * i, 512) for i in range(4)]
LATBL = BL[1:]


class Sem:
    def __init__(self, h, sid):
        self.h = h
        self.total = 0
        self.id = sid


class Buf:
    __slots__ = ("name", "last_w", "reads", "dsem")

    def __init__(self, name):
        self.name = name
        self.last_w = None
        self.reads = {}
        self.dsem = None


class Kern:
    def __init__(self, nc, es):
        self.nc = nc
        self.es = es
        self.engs = {"pe": nc.tensor, "act": nc.scalar, "dve": nc.vector,
                     "pool": nc.gpsimd, "sp": nc.sync}
        self.nsem = 0
        self.prog = {e: self.sem("prog_" + e) for e in ("pe", "act", "dve", "pool")}
        self.known = {e: {} for e in self.engs}
        self.bufs = {}
        self.ninst = {e: 0 for e in self.engs}
        self.dsems = []

    def sem(self, name):
        h = self.es.enter_context(self.nc.semaphore(name))
        self.nsem += 1
        return Sem(h, self.nsem)

    def B(self, *key):
        b = self.bufs.get(key)
        if b is None:
            b = Buf(key)
            self.bufs[key] = b
        return b

    def wait(self, e, sem, val):
        k = self.known[e]
        if k.get(sem.id, 0) >= val:
            return
        self.engs[e].wait_ge(sem.h, val)
        k[sem.id] = val

    def _deps(self, reads, writes):
        deps = {}

        def add(tok):
            s, v = tok
            if s.id not in deps or deps[s.id][1] < v:
                deps[s.id] = (s, v)
        for b in reads:
            if b.last_w is not None:
                add(b.last_w)
        for b in writes:
            if b.last_w is not None:
                add(b.last_w)
            for tok in b.reads.values():
                add(tok)
        return deps.values()

    def _record(self, tok, reads, writes):
        s, v = tok
        for b in reads:
            old = b.reads.get(s.id)
            if old is None or old[1] < v:
                b.reads[s.id] = tok
        for b in writes:
            b.last_w = tok
            b.reads = {}

    def op(self, e, fn, reads=(), writes=(), sig=True):
        ps = self.prog[e]
        for (s, v) in self._deps(reads, writes):
            if e == "pe" and s is ps:
                continue
            self.wait(e, s, v)
        ins = fn(self.engs[e])
        self.ninst[e] += 1
        if sig:
            ps.total += 1
            ins.then_inc(ps.h, 1)
            tok = (ps, ps.total)
        else:
            tok = (ps, ps.total + 1)
        self._record(tok, reads, writes)
        return ins

    def dma(self, q, out, in_, reads=(), writes=(), track=True):
        for (s, v) in self._deps(reads, writes):
            self.wait(q, s, v)
        b0 = writes[0]
        if b0.dsem is None:
            b0.dsem = self.sem("d%d" % self.nsem)
            if track:
                self.dsems.append(b0.dsem)
        ds = b0.dsem
        ins = self.engs[q].dma_start(out=out, in_=in_)
        ds.total += 16
        ins.then_inc(ds.h, 16)
        self.ninst[q] += 1
        self._record((ds, ds.total), reads, writes)
        return ins

    def collective(self, kind, groups, in_ap, out_ap, reads, writes, alu=None, track=True):
        for (s, v) in self._deps(reads, writes):
            self.wait("pool", s, v)
        cs = self.sem("cc%d" % self.nsem)
        ins = self.nc.gpsimd.collective_compute(kind, alu if alu is not None else ALU.bypass, replica_groups=groups,
                                                ins=[in_ap], outs=[out_ap])
        ins.then_inc(cs.h, 1)
        cs.total = 1
        if track:
            self.dsems.append(cs)
        self._record((cs, 1), reads, writes)

    def barrier(self):
        for e in self.engs:
            for p in self.prog.values():
                if p.total > 0:
                    self.wait(e, p, p.total)
            for ds in self.dsems:
                if ds.total > 0:
                    self.wait(e, ds, ds.total)

    def wait_all(self, e, bufs):
        for b in bufs:
            if b.last_w is not None:
                self.wait(e, b.last_w[0], b.last_w[1])


VEC = {}


def _vec_layout():
    off = 0
    for name, n in [("sc5", 5), ("oh", 4), ("gmix0", 8), ("gffn0", 8), ("gmix1", 8), ("gffn1", 8),
                    ("adab0", 48), ("adab1", 48),
                    ("convw", 40), ("convb", 8), ("gateb", 32), ("lam", 16),
                    ("mask", 2), ("qg", 3), ("kvg", 2), ("fg", 8)]:
        VEC[name] = off
        off += n
    return off


NV = _vec_layout()

DV = {}


def _dv_layout():
    off = 0
    for name, n in [("scs", 8), ("part", 480), ("red", 480), ("mod", 192), ("gsmix0", 16), ("gsffn0", 16),
                    ("gsmix1", 16), ("gsffn1", 16), ("cl", 16), ("cl2", 16),
                    ("nb", 32), ("tmp", 16), ("fin", 2), ("hinit", 2), ("gath", 4)]:
        DV[name] = off
        off += n
    return off


ND = _dv_layout()


def fm(v):
    v = np.asarray(v, np.float32)
    lead = v.shape[:-1]
    n = v.shape[-1] // 128
    w = v.reshape(lead + (n, 128))
    w = np.moveaxis(w, -1, 0)
    return np.ascontiguousarray(w)


class NS:
    pass


def modcol(i, j, c, t):
    return DV["mod"] + i * 96 + (j * 8 + c) * 2 + t


def mm(K, ps_ap, lhsT, rhs, start, stop, reads, writes):
    return K.op("pe", lambda e: e.matmul(ps_ap, lhsT, rhs, start=start, stop=stop),
                reads=reads, writes=writes, sig=stop)


class PsumRing:
    def __init__(self, K, banks, idx):
        self.K = K
        self.banks = banks
        self.idx = list(idx)
        self.i = 0

    def next(self):
        b = self.idx[self.i]
        self.i = (self.i + 1) % len(self.idx)
        return self.banks[b], self.K.B("psum", b)


def gather_weights(K, T, names):
    for name in names:
        sh, bounce, full = T.d_sh[name], T.d_bounce[name], T.d_full[name]
        K.dma("sp", bounce, sh, writes=[K.B("bounce", name)], track=False)
        K.collective("AllGather", [list(range(8))], bounce, full, reads=[K.B("bounce", name)], writes=[K.B("W", name)], track=False)


def phase_prep(K, nc, T, es):
    dv, vecs = T.dv, T.vecs
    Bv, Bd = K.B("vecs"), K.B("dv")
    K.dma("sp", vecs[:], T.d_vecs, writes=[Bv])
    K.op("pool", lambda e: e.memset(T.ones[:], 1.0), writes=[K.B("ones")])
    o = VEC["sc5"]
    K.op("act", lambda e: e.activation(out=dv[:, DV["scs"]:DV["scs"] + 5], in_=vecs[:, o:o + 5], func=AF.Silu),
         reads=[Bv], writes=[K.B("scs")])
    po = DV["part"]
    with ExitStack() as les:
        slab = [les.enter_context(nc.sbuf_tensor("adaslab%d" % i, [128, 6 * D], F32, side="right")) for i in range(2)]
        for i in range(2):
            Bs = K.B("adaslab", i)
            K.dma("sp", slab[i][:], T.d_adaw[i], writes=[Bs])
            ps, Bp = T.psum.next()
            for g in range(48):
                mm(K, ps[:, g * 5:g * 5 + 5], slab[i][:, g * 128:(g + 1) * 128], dv[:, DV["scs"]:DV["scs"] + 5], True, True,
                   reads=[Bs, K.B("scs")], writes=[Bp])
            K.op("dve", lambda e: e.tensor_copy(out=dv[:, po + i * 240:po + (i + 1) * 240], in_=ps[:, 0:240]),
                 reads=[Bp], writes=[K.B("part")])
        K.dma("sp", T.d_arin, dv[:, po:po + 480], reads=[K.B("part")], writes=[K.B("arin")])
        K.collective("AllReduce", [list(range(8))], T.d_arin, T.d_arout, reads=[K.B("arin")], writes=[K.B("arout")], alu=ALU.add)
        gather_weights(K, T, T.wearly)
        ro = DV["red"]
        K.dma("sp", dv[:, ro:ro + 480], T.d_arout, reads=[K.B("arout")], writes=[K.B("red")])
        K.barrier()
    mo, oh, ab = DV["mod"], VEC["oh"], VEC["adab0"]
    K.op("dve", lambda e: e.tensor_scalar(out=dv[:, mo:mo + 192:2], in0=dv[:, ro:ro + 480:5], scalar1=vecs[:, oh:oh + 1],
                                          scalar2=None, op0=ALU.mult), reads=[K.B("red"), Bv], writes=[Bd])
    for q in range(1, 4):
        K.op("dve", lambda e: e.scalar_tensor_tensor(out=dv[:, mo:mo + 192:2], in0=dv[:, ro + q:ro + 480:5], scalar=vecs[:, oh + q:oh + q + 1],
                                                     in1=dv[:, mo:mo + 192:2], op0=ALU.mult, op1=ALU.add),
             reads=[K.B("red"), Bv, Bd], writes=[Bd])
    K.op("dve", lambda e: e.tensor_tensor(out=dv[:, mo:mo + 192:2], in0=dv[:, mo:mo + 192:2], in1=vecs[:, ab:ab + 96], op=ALU.add),
         reads=[Bd, Bv], writes=[Bd])
    K.op("dve", lambda e: e.tensor_tensor(out=dv[:, mo + 1:mo + 192:2], in0=dv[:, ro + 4:ro + 480:5], in1=vecs[:, ab:ab + 96], op=ALU.add),
         reads=[K.B("red"), Bv, Bd], writes=[Bd])
    for i in range(2):
        for gname, sj, oname in (("gmix", 1, "gsmix"), ("gffn", 4, "gsffn")):
            for t in range(2):
                src = modcol(i, sj, 0, t)
                dst = DV["%s%d" % (oname, i)] + t * 8
                gsrc = VEC["%s%d" % (gname, i)]
                K.op("dve", lambda e, src=src, dst=dst, gsrc=gsrc: e.scalar_tensor_tensor(
                    out=dv[:, dst:dst + 8], in0=dv[:, src:src + 16:2], scalar=1.0, in1=vecs[:, gsrc:gsrc + 8],
                    op0=ALU.add, op1=ALU.mult), reads=[Bd, Bv], writes=[Bd])
    lo, to = VEC["lam"], DV["tmp"]
    K.op("act", lambda e: e.activation(out=dv[:, to:to + 16], in_=vecs[:, lo:lo + 16], func=AF.Exp, scale=-1.0),
         reads=[Bv], writes=[Bd])
    K.op("act", lambda e: e.activation(out=dv[:, to:to + 16], in_=dv[:, to:to + 16], func=AF.Ln, bias=1.0, scale=1.0),
         reads=[Bd], writes=[Bd])
    K.op("dve", lambda e: e.tensor_scalar(out=dv[:, DV["cl"]:DV["cl"] + 16], in0=dv[:, to:to + 16], scalar1=-8.0,
                                          scalar2=None, op0=ALU.mult), reads=[Bd], writes=[Bd])
    K.op("dve", lambda e: e.tensor_scalar(out=dv[:, DV["cl2"]:DV["cl2"] + 16], in0=dv[:, to:to + 16], scalar1=-16.0,
                                          scalar2=None, op0=ALU.mult), reads=[Bd], writes=[Bd])
    go = VEC["gateb"]
    K.op("dve", lambda e: e.tensor_scalar(out=dv[:, DV["nb"]:DV["nb"] + 32], in0=vecs[:, go:go + 32], scalar1=-1.0,
                                          scalar2=None, op0=ALU.mult), reads=[Bv], writes=[Bd])


def norm_block(K, T, x3, xbufs, n, gs_col, shift_col, outs, tag):
    nt = T.nrm
    k = nt.cnt
    nt.cnt += 1
    sq, Bsq = nt.sq[k % len(nt.sq)], K.B("nsq", k % len(nt.sq))
    rt, Brt = nt.rt[k % len(nt.rt)], K.B("nrt", k % len(nt.rt))
    Bv, Bd = K.B("vecs"), K.B("dv")
    K.op("act", lambda e: e.activation(out=sq[:, 0:x3.shape[1], :n], in_=x3, func=AF.Square), reads=xbufs, writes=[Bsq])
    ps, Bp = T.psum.next()
    nch = x3.shape[1]
    for c in range(nch):
        mm(K, ps[:, :n], T.ones[:], sq[:, c, :n], c == 0, c == nch - 1, reads=[K.B("ones"), Bsq], writes=[Bp])
    K.op("act", lambda e: e.activation(out=rt[:, :n], in_=ps[:, :n], func=AF.Ln, bias=T.epsc[:, 0:1], scale=1.0 / (128 * nch)),
         reads=[Bp, K.B("epsc")], writes=[Brt])
    K.op("act", lambda e: e.activation(out=rt[:, :n], in_=rt[:, :n], func=AF.Exp, scale=-0.5), reads=[Brt], writes=[Brt])
    for c in range(nch):
        o_ap, o_buf = outs(c)
        if shift_col is None:
            K.op("dve", lambda e, c=c, o_ap=o_ap: e.scalar_tensor_tensor(
                out=o_ap, in0=x3[:, c, :], scalar=gs_col(c), in1=rt[:, :n], op0=ALU.mult, op1=ALU.mult),
                reads=xbufs + [Brt, Bv, Bd], writes=[o_buf])
        else:
            j = nt.tcnt
            nt.tcnt += 1
            tmp, Bt = nt.tmp[j % len(nt.tmp)], K.B("ntmp", j % len(nt.tmp))
            K.op("dve", lambda e, c=c, tmp=tmp: e.scalar_tensor_tensor(
                out=tmp[:, :n], in0=x3[:, c, :], scalar=gs_col(c), in1=rt[:, :n], op0=ALU.mult, op1=ALU.mult),
                reads=xbufs + [Brt, Bv, Bd], writes=[Bt])
            K.op("act", lambda e, c=c, tmp=tmp, o_ap=o_ap: e.activation(
                out=o_ap, in_=tmp[:, :n], func=AF.Identity, bias=shift_col(c), scale=1.0),
                reads=[Bt, Bd], writes=[o_buf])


def layer0_mixer(K, nc, T, es):
    dv, vecs = T.dv, T.vecs
    Bv, Bd = K.B("vecs"), K.B("dv")
    yg = T.yg
    with ExitStack() as les:
        R = lambda name, shape, dt: les.enter_context(nc.sbuf_tensor(name, shape, dt, side="right"))
        h = R("h0", [128, 8, TOK], BF16)
        hh = R("h0halo", [128, 8, 2], BF16)
        with ExitStack() as nes:
            N_ = lambda name, shape, dt: nes.enter_context(nc.sbuf_tensor(name, shape, dt, side="right"))
            xin = [N_("xin%d" % i, [128, 8, 512], F32) for i in range(2)]
            T.nrm = NS()
            T.nrm.cnt = 0
            T.nrm.tcnt = 0
            T.nrm.sq = [N_("nsq%d" % i, [128, 8, 512], BF16) for i in range(2)]
            T.nrm.rt = [N_("nrt%d" % i, [128, 512], F32) for i in range(2)]
            T.nrm.tmp = [N_("ntmp%d" % i, [128, 512], F32) for i in range(4)]
            blocks = [(t0, n, False) for (t0, n) in BL] + [(TOK, 2, True)]
            for bi, (t0, n, halo) in enumerate(blocks):
                xi, Bx = xin[bi % 2], K.B("xin", bi % 2)
                if t0 < CTX:
                    src, t = T.d_ctxT[:, :, t0:t0 + n], 1
                else:
                    src, t = T.d_xT[:, :, t0 - CTX:t0 - CTX + n], 0
                K.dma("sp", xi[:, :, :n], src, writes=[Bx])
                gs = DV["gsmix0"] + t * 8

                def outs(c, t0=t0, n=n, halo=halo):
                    if halo:
                        return hh[:, c, :], K.B("hh", c)
                    return h[:, c, t0:t0 + n], K.B("h", c, t0)
                norm_block(K, T, xi[:, :, :n], [Bx], n,
                           lambda c, gs=gs: dv[:, gs + c:gs + c + 1],
                           lambda c, t=t: dv[:, modcol(0, 0, c, t):modcol(0, 0, c, t) + 1],
                           outs, "n0")
        U = [R("U%d" % i, [128, TOK + 8], F32) for i in range(2)]
        xc32 = [R("xc32_%d" % i, [128, TOK], F32) for i in range(2)]
        xc16 = [R("xc16_%d" % i, [128, TOK], BF16) for i in range(2)]
        gn = [R("gn%d" % i, [128, TOK], BF16) for i in range(2)]
        yn = [R("yn%d" % i, [128, TOK], F32) for i in range(2)]
        wu = [R("wu%d" % i, [128, 8, 256], BF16) for i in range(1)]
        wg = [R("wg%d" % i, [128, 8, 256], BF16) for i in range(1)]
        gw = [R("gw%d" % i, [128, 8, 256], BF16) for i in range(1)]
        NT = 3
        ta = [R("ta%d" % i, [128, 512], F32) for i in range(NT)]
        t1 = [R("t1_%d" % i, [128, 512], F32) for i in range(NT)]
        t2 = [R("t2_%d" % i, [128, 512], F32) for i in range(NT)]
        t3 = [R("t3_%d" % i, [128, 512], F32) for i in range(NT)]
        hs2 = [R("hs2_%d" % i, [128, 512], F32) for i in range(NT)]
        UC0, UL0 = 0, 260
        win3 = T.d_lru_w_in.rearrange("(kc p) n -> p kc n", p=128)

        def load_w(n):
            s = 0
            K.dma("pool", wu[s][:], win3[:, :, 1024 + n * 256:1024 + (n + 1) * 256], reads=[K.B("W", "lru_w_in")], writes=[K.B("wu", s)])
            K.dma("pool", wg[s][:], win3[:, :, n * 256:(n + 1) * 256], reads=[K.B("W", "lru_w_in")], writes=[K.B("wg", s)])
            K.dma("pool", gw[s][:].rearrange("p a b -> p (a b)"), T.d_lru_gw[n], writes=[K.B("gw", s)])
        tcnt = 0
        for n in range(4):
            s = 0
            load_w(n)
            Bwu, Bwg, Bgw = K.B("wu", s), K.B("wg", s), K.B("gw", s)
            hbufs_blk = lambda t0: [K.B("h", c, t0) for c in range(8)]
            for oc in range(2):
                ch = n * 2 + oc
                K.op("dve", lambda e, oc=oc: e.memset(U[oc][:, 0:2], 0.0), writes=[K.B("U", oc, "pad")])
                K.op("dve", lambda e, oc=oc: e.memset(U[oc][:, 258:262], 0.0), writes=[K.B("U", oc, "pad")])
                for (t0, nn) in BL + [(TOK, 2)]:
                    ps, Bp = T.psum.next()
                    halo = t0 == TOK
                    for kc in range(8):
                        rhs = hh[:, kc, :] if halo else h[:, kc, t0:t0 + nn]
                        rb = [K.B("hh", kc)] if halo else [K.B("h", kc, t0)]
                        mm(K, ps[:, :nn], wu[s][:, kc, oc * 128:(oc + 1) * 128], rhs, kc == 0, kc == 7,
                           reads=[Bwu] + rb, writes=[Bp])
                    ucol = (UC0 + 2 + t0) if t0 < CTX else (UL0 + 2 + t0 - CTX)
                    K.op("dve", lambda e, oc=oc, ucol=ucol, nn=nn, ps=ps: e.tensor_copy(out=U[oc][:, ucol:ucol + nn], in_=ps[:, :nn]),
                         reads=[Bp], writes=[K.B("U", oc, t0)])
                for (t0, nn) in BL:
                    ps, Bp = T.psum.next()
                    for kc in range(8):
                        mm(K, ps[:, :nn], wg[s][:, kc, oc * 128:(oc + 1) * 128], h[:, kc, t0:t0 + nn], kc == 0, kc == 7,
                           reads=[Bwg, K.B("h", kc, t0)], writes=[Bp])
                    K.op("act", lambda e, oc=oc, t0=t0, nn=nn, ps=ps: e.activation(out=gn[oc][:, t0:t0 + nn], in_=ps[:, :nn],
                                                                                func=AF.Gelu_apprx_tanh),
                         reads=[Bp], writes=[K.B("gn", oc, t0)])
                cw = VEC["convw"] + ch * 5
                cb = VEC["convb"] + ch
                allU = [K.B("U", oc, t0) for (t0, nn) in BL + [(TOK, 2)]] + [K.B("U", oc, "pad")]
                for (t0, nn) in BL:
                    ub = (UC0 + t0) if t0 < CTX else (UL0 + t0 - CTX)
                    Bx32 = K.B("xc32", oc, t0)
                    K.op("dve", lambda e: e.tensor_scalar(
                        out=xc32[oc][:, t0:t0 + nn], in0=U[oc][:, ub:ub + nn], scalar1=vecs[:, cw:cw + 1],
                        scalar2=vecs[:, cb:cb + 1], op0=ALU.mult, op1=ALU.add), reads=allU + [Bv], writes=[Bx32])
                    for k in range(1, 5):
                        K.op("dve", lambda e: e.scalar_tensor_tensor(
                            out=xc32[oc][:, t0:t0 + nn], in0=U[oc][:, ub + k:ub + k + nn], scalar=vecs[:, cw + k:cw + k + 1],
                            in1=xc32[oc][:, t0:t0 + nn], op0=ALU.mult, op1=ALU.add),
                            reads=allU + [Bv, Bx32], writes=[Bx32])
                    K.op("act", lambda e: e.activation(out=xc16[oc][:, t0:t0 + nn], in_=xc32[oc][:, t0:t0 + nn], func=AF.Copy),
                         reads=[Bx32], writes=[K.B("xc16", oc, t0)])
            for slot in range(2):
                if slot == 1:
                    fo = DV["fin"]
                    K.dma("sp", T.d_cin[n], dv[:, fo:fo + 2], reads=[K.B("fin")], writes=[K.B("cin", n)])
                    K.collective("AllGather", T.groups, T.d_cin[n], T.d_cout[n], reads=[K.B("cin", n)], writes=[K.B("cout", n)])
                    go = DV["gath"]
                    K.dma("sp", dv[:, go:go + 4].rearrange("p (r c) -> p r c", r=2),
                          T.d_cout[n].rearrange("(r p) c -> p r c", p=128), reads=[K.B("cout", n)], writes=[K.B("gath")])
                    mo, ho = VEC["mask"], DV["hinit"]
                    K.op("dve", lambda e: e.tensor_scalar(out=dv[:, ho:ho + 2], in0=dv[:, go:go + 2], scalar1=vecs[:, mo:mo + 1],
                                                          scalar2=None, op0=ALU.mult), reads=[K.B("gath"), Bv], writes=[K.B("hinit")])
                    K.op("dve", lambda e: e.scalar_tensor_tensor(out=dv[:, ho:ho + 2], in0=dv[:, go + 2:go + 4], scalar=vecs[:, mo + 1:mo + 2],
                                                                 in1=dv[:, ho:ho + 2], op0=ALU.mult, op1=ALU.add),
                         reads=[K.B("gath"), Bv, K.B("hinit")], writes=[K.B("hinit")])
                for oc in range(2):
                    ch = n * 2 + oc
                    By = K.B("yn", oc)
                    nbr = DV["nb"] + (slot * 2 + 0) * 8 + ch
                    nbi = DV["nb"] + (slot * 2 + 1) * 8 + ch
                    clc = DV["cl"] + slot * 8 + ch
                    cl2c = DV["cl2"] + slot * 8 + ch
                    order = BL if slot == 0 else [BL[0]] + LATBL[::-1]
                    prev = None
                    for (t0, nn) in order:
                        k = tcnt % NT
                        tcnt += 1
                        psr, Bpr = T.psum.next()
                        psi, Bpi = T.psum.next()
                        for gate, ps, Bp in ((0, psr, Bpr), (1, psi, Bpi)):
                            for kc in range(2):
                                mm(K, ps[:, :nn], gw[s][:, (slot * 2 + gate) * 2 + kc, oc * 128:(oc + 1) * 128],
                                   xc16[kc][:, t0:t0 + nn], kc == 0, kc == 1, reads=[Bgw, K.B("xc16", kc, t0)], writes=[Bp])
                        A, T1, T2, T3 = ta[k], t1[k], t2[k], t3[k]
                        BA, B1, B2, B3 = K.B("ta", k), K.B("t1", k), K.B("t2", k), K.B("t3", k)
                        sl = slice(0, nn)
                        K.op("act", lambda e: e.activation(out=T1[:, sl], in_=psr[:, sl], func=AF.Exp, bias=dv[:, nbr:nbr + 1], scale=-1.0),
                             reads=[Bpr, Bd], writes=[B1])
                        K.op("act", lambda e: e.activation(out=T1[:, sl], in_=T1[:, sl], func=AF.Ln, bias=1.0, scale=1.0),
                             reads=[B1], writes=[B1])
                        K.op("act", lambda e: e.activation(out=T1[:, sl], in_=T1[:, sl], func=AF.Exp, scale=-1.0),
                             reads=[B1], writes=[B1])
                        K.op("act", lambda e: e.activation(out=A[:, sl], in_=T1[:, sl], func=AF.Exp, scale=dv[:, clc:clc + 1]),
                             reads=[B1, Bd], writes=[BA])
                        K.op("act", lambda e: e.activation(out=T2[:, sl], in_=T1[:, sl], func=AF.Exp, scale=dv[:, cl2c:cl2c + 1]),
                             reads=[B1, Bd], writes=[B2])
                        K.op("act", lambda e: e.activation(out=T2[:, sl], in_=T2[:, sl], func=AF.Ln, bias=1.0, scale=-1.0),
                             reads=[B2], writes=[B2])
                        K.op("act", lambda e: e.activation(out=T2[:, sl], in_=T2[:, sl], func=AF.Exp, scale=0.5),
                             reads=[B2], writes=[B2])
                        K.op("act", lambda e: e.activation(out=T3[:, sl], in_=psi[:, sl], func=AF.Exp, bias=dv[:, nbi:nbi + 1], scale=-1.0),
                             reads=[Bpi, Bd], writes=[B3])
                        K.op("act", lambda e: e.activation(out=T3[:, sl], in_=T3[:, sl], func=AF.Ln, bias=1.0, scale=1.0),
                             reads=[B3], writes=[B3])
                        K.op("act", lambda e: e.activation(out=T3[:, sl], in_=T3[:, sl], func=AF.Exp, scale=-1.0),
                             reads=[B3], writes=[B3])
                        K.op("dve", lambda e: e.tensor_tensor(out=T3[:, sl], in0=T3[:, sl], in1=xc32[oc][:, t0:t0 + nn], op=ALU.mult),
                             reads=[B3, K.B("xc32", oc, t0)], writes=[B3])
                        K.op("dve", lambda e: e.tensor_tensor(out=T3[:, sl], in0=T3[:, sl], in1=T2[:, sl], op=ALU.mult),
                             reads=[B3, B2], writes=[B3])
                        if t0 < CTX:
                            init, ib = 0.0, []
                        elif slot == 0:
                            init, ib = yn[oc][:, t0 - 1:t0], [By]
                        elif prev is None or prev[2]:
                            ho = DV["hinit"] + oc
                            init, ib = dv[:, ho:ho + 1], [K.B("hinit")]
                        else:
                            init, ib = prev[0], [prev[1]]
                        if slot == 0:
                            K.op("dve", lambda e: e.tensor_tensor_scan(out=yn[oc][:, t0:t0 + nn], data0=A[:, sl], data1=T3[:, sl],
                                                                       initial=init, op0=ALU.mult, op1=ALU.add),
                                 reads=[BA, B3] + ib, writes=[By])
                            prev = None
                        else:
                            H2, BH = hs2[k], K.B("hs2", k)
                            K.op("dve", lambda e: e.tensor_tensor_scan(out=H2[:, sl][:, ::-1], data0=A[:, sl][:, ::-1], data1=T3[:, sl][:, ::-1],
                                                                       initial=init, op0=ALU.mult, op1=ALU.add),
                                 reads=[BA, B3] + ib, writes=[BH])
                            prev = (H2[:, 0:1], BH, t0 < CTX)
                            K.op("dve", lambda e: e.tensor_tensor(out=yn[oc][:, t0:t0 + nn], in0=yn[oc][:, t0:t0 + nn], in1=H2[:, sl], op=ALU.add),
                                 reads=[BH, By], writes=[By])
                            K.op("dve", lambda e: e.tensor_tensor(out=yg[:, ch, t0:t0 + nn], in0=yn[oc][:, t0:t0 + nn], in1=gn[oc][:, t0:t0 + nn], op=ALU.mult),
                                 reads=[By, K.B("gn", oc, t0)], writes=[K.B("yg", ch, t0)])
                    if slot == 0:
                        fo = DV["fin"] + oc
                        K.op("dve", lambda e: e.tensor_copy(out=dv[:, fo:fo + 1], in_=yn[oc][:, TOK - 1:TOK]), reads=[By], writes=[K.B("fin")])


def out_proj(K, nc, T, wname, src, i, blocks, load_x):
    dv = T.dv
    Bd = K.B("dv")
    with ExitStack() as les:
        R = lambda name, shape, dt: les.enter_context(nc.sbuf_tensor(name, shape, dt, side="right"))
        wo = R("wo", [128, 8, 1024], BF16)
        Bw = K.B("wo")
        w3 = T.d_full[wname].rearrange("(kc p) n -> p kc n", p=128)
        for kc in range(8):
            K.dma("pool", wo[:, kc, :], w3[:, kc, :], reads=[K.B("W", wname)], writes=[K.B("wo", kc)])
        xin = [R("xin_o%d" % j, [128, 8, 512], F32) for j in range(2)] if load_x else None
        for bi, (t0, nn) in enumerate(blocks):
            t = 1 if t0 < CTX else 0
            if load_x:
                xi, Bx = xin[bi % 2], K.B("xin_o", bi % 2)
                srcx = T.d_ctxT[:, :, t0:t0 + nn] if t0 < CTX else T.d_xT[:, :, t0 - CTX:t0 - CTX + nn]
                K.dma("sp", xi[:, :, :nn], srcx, writes=[Bx])
            for oc in range(8):
                ps, Bp = T.psum.next()
                for kc in range(8):
                    mm(K, ps[:, :nn], wo[:, kc, oc * 128:(oc + 1) * 128], src[:, kc, t0:t0 + nn], kc == 0, kc == 7,
                       reads=[K.B("wo", kc), K.B("yg", kc, t0)], writes=[Bp])
                gc = modcol(i, 2, oc, t)
                Bxr = K.B("x", oc, t0)
                if load_x:
                    K.op("dve", lambda e: e.scalar_tensor_tensor(out=T.x_res[:, oc, t0:t0 + nn], in0=ps[:, :nn], scalar=dv[:, gc:gc + 1],
                                                                 in1=xi[:, oc, :nn], op0=ALU.mult, op1=ALU.add),
                         reads=[Bp, Bd, Bx], writes=[Bxr])
                else:
                    K.op("dve", lambda e: e.scalar_tensor_tensor(out=T.x_res[:, oc, t0:t0 + nn], in0=ps[:, :nn], scalar=dv[:, gc:gc + 1],
                                                                 in1=T.x_res[:, oc, t0:t0 + nn], op0=ALU.mult, op1=ALU.add),
                         reads=[Bp, Bd, Bxr], writes=[Bxr])
    K.barrier()


def ffn(K, nc, T, i, groups):
    dv = T.dv
    Bd = K.B("dv")
    win5 = T.d_ffn_w_in[i].rearrange("(kc p) (s j q) -> p kc s j q", p=128, s=2, j=NHC, q=128)
    wout3 = T.d_ffn_w_out[i].rearrange("(kc p) n -> p kc n", p=128)
    maxtok = max(sum(nn for _, nn in g) for g in groups)
    with ExitStack() as les:
        R = lambda name, shape, dt: les.enter_context(nc.sbuf_tensor(name, shape, dt, side="right"))
        h2 = R("h2_%d" % i, [128, 8, maxtok], BF16)
        hact = R("hact_%d" % i, [128, NHC, maxtok], BF16)
        for gi, g in enumerate(groups):
            offs = []
            o = 0
            for (_, nn) in g:
                offs.append(o)
                o += nn
            tag = "%d_%d" % (i, gi)
            with ExitStack() as nes:
                N_ = lambda name, shape, dt: nes.enter_context(nc.sbuf_tensor(name + tag, shape, dt, side="right"))
                T.nrm = NS()
                T.nrm.cnt = 0
                T.nrm.tcnt = 0
                T.nrm.sq = [N_("fsq%d" % j, [128, 8, 512], BF16) for j in range(2)]
                T.nrm.rt = [N_("frt%d" % j, [128, 512], F32) for j in range(2)]
                T.nrm.tmp = [N_("ftmp%d" % j, [128, 512], F32) for j in range(4)]
                for (t0, nn), lo in zip(g, offs):
                    t = 1 if t0 < CTX else 0
                    gs = DV["gsffn%d" % i] + t * 8
                    norm_block(K, T, T.x_res[:, :, t0:t0 + nn], [K.B("x", c, t0) for c in range(8)], nn,
                               lambda c: dv[:, gs + c:gs + c + 1],
                               lambda c: dv[:, modcol(i, 3, c, t):modcol(i, 3, c, t) + 1],
                               lambda c: (h2[:, c, lo:lo + nn], K.B("h2", c, lo)), "f")
                K.barrier()
            with ExitStack() as wes:
                W_ = lambda name, shape, dt: wes.enter_context(nc.sbuf_tensor(name + tag, shape, dt, side="right"))
                w1 = [W_("w1_%d" % j, [128, 8, 2, 128], BF16) for j in range(2)]
                w2 = [W_("w2_%d" % j, [128, NHC, 128], BF16) for j in range(2)]
                sg = [W_("sg%d" % j, [128, 512], F32) for j in range(3)]
                sc = 0
                for j in range(NHC):
                    sl = j % 2
                    for s_ in range(2):
                        K.dma("pool", w1[sl][:, :, s_, :], win5[:, :, s_, j, :], reads=[K.B("W", "ffn_w_in%d" % i)], writes=[K.B("w1", sl, s_)])
                    for (t0, nn), lo in zip(g, offs):
                        psg, Bpg = T.psum.next()
                        psu, Bpu = T.psum.next()
                        for s_, ps, Bp in ((0, psg, Bpg), (1, psu, Bpu)):
                            for kc in range(8):
                                mm(K, ps[:, :nn], w1[sl][:, kc, s_, :], h2[:, kc, lo:lo + nn], kc == 0, kc == 7,
                                   reads=[K.B("w1", sl, s_), K.B("h2", kc, lo)], writes=[Bp])
                        sgt, Bsg = sg[sc % 3], K.B("sg", sc % 3)
                        sc += 1
                        K.op("act", lambda e: e.activation(out=sgt[:, :nn], in_=psg[:, :nn], func=AF.Silu), reads=[Bpg], writes=[Bsg])
                        K.op("dve", lambda e: e.tensor_tensor(out=hact[:, j, lo:lo + nn], in0=sgt[:, :nn], in1=psu[:, :nn], op=ALU.mult),
                             reads=[Bsg, Bpu], writes=[K.B("hact", j, lo)])
                for oc in range(8):
                    sl = oc % 2
                    K.dma("pool", w2[sl][:, 0:11, :], wout3[:, 0:11, oc * 128:(oc + 1) * 128], reads=[K.B("W", "ffn_w_out%d" % i)], writes=[K.B("w2", sl, 0)])
                    K.dma("pool", w2[sl][:, 11:22, :], wout3[:, 11:22, oc * 128:(oc + 1) * 128], reads=[K.B("W", "ffn_w_out%d" % i)], writes=[K.B("w2", sl, 1)])
                    for (t0, nn), lo in zip(g, offs):
                        t = 1 if t0 < CTX else 0
                        ps, Bp = T.psum.next()
                        for kc in range(NHC):
                            mm(K, ps[:, :nn], w2[sl][:, kc, :], hact[:, kc, lo:lo + nn], kc == 0, kc == NHC - 1,
                               reads=[K.B("w2", sl, kc // 11), K.B("hact", kc, lo)], writes=[Bp])
                        gc = modcol(i, 5, oc, t)
                        Bxr = K.B("x", oc, t0)
                        K.op("dve", lambda e: e.scalar_tensor_tensor(out=T.x_res[:, oc, t0:t0 + nn], in0=ps[:, :nn], scalar=dv[:, gc:gc + 1],
                                                                     in1=T.x_res[:, oc, t0:t0 + nn], op0=ALU.mult, op1=ALU.add),
                             reads=[Bp, Bd, Bxr], writes=[Bxr])
                K.barrier()


KEYBL = [(512 * i, 512) for i in range(8)] + [(4096, 256)]
NTILE = NKEY // 128


def make_nrm(nc, scope, tag, nsq, nrt, ntmp):
    n = NS()
    n.cnt = 0
    n.tcnt = 0
    A_ = lambda name, shape, dt: scope.enter_context(nc.sbuf_tensor(name + tag, shape, dt, side="right"))
    n.sq = [A_("nsq%d" % j, [128, 8, 512], BF16) for j in range(nsq)]
    n.rt = [A_("nrt%d" % j, [128, 512], F32) for j in range(nrt)]
    n.tmp = [A_("ntmp%d" % j, [128, 512], F32) for j in range(ntmp)]
    return n


def layer1_mixer(K, nc, T, es):
    dv, vecs = T.dv, T.vecs
    Bv, Bd = K.B("vecs"), K.B("dv")
    with ExitStack() as les:
        R = lambda name, shape, dt: les.enter_context(nc.sbuf_tensor(name, shape, dt, side="right"))
        KL = R("KL", [128, 3, NKEY], BF16)
        w_in3 = T.d_full["mla_w_in"].rearrange("(kc p) n -> p kc n", p=128)
        xb = lambda t0: [K.B("x", c, t0) for c in range(8)]
        with ExitStack() as kes:
            S_ = lambda name, shape, dt: kes.enter_context(nc.sbuf_tensor(name, shape, dt, side="right"))
            wkv = S_("wkv", [128, 8, 512], BF16)
            K.dma("pool", wkv[:], w_in3[:, :, 384:896], reads=[K.B("W", "mla_w_in")], writes=[K.B("wkv")])
            hb = S_("hbK", [128, 8, 512], BF16)
            ckv = S_("ckvK", [128, 2, 512], F32)
            krb = S_("krK", [128, 2, 512], F32)
            tb = S_("tbK", [128, 2, 512], F32)
            ra = S_("raK", [128, 2, 512], F32)
            T.nrm = make_nrm(nc, kes, "K", 1, 2, 2)
            for (t0, nn) in BL:
                t = 1 if t0 < CTX else 0
                gs = DV["gsmix1"] + t * 8
                K.dma("sp", tb[:, :, :nn], T.d_cs[:, :, t0:t0 + nn], writes=[K.B("tb")])
                norm_block(K, T, T.x_res[:, :, t0:t0 + nn], xb(t0), nn,
                           lambda c: dv[:, gs + c:gs + c + 1],
                           lambda c: dv[:, modcol(1, 0, c, t):modcol(1, 0, c, t) + 1],
                           lambda c: (hb[:, c, :nn], K.B("hb", c)), "k")
                for oc in range(4):
                    ps, Bp = T.psum.next()
                    for kc in range(8):
                        mm(K, ps[:, :nn], wkv[:, kc, oc * 128:(oc + 1) * 128], hb[:, kc, :nn], kc == 0, kc == 7,
                           reads=[K.B("wkv"), K.B("hb", kc)], writes=[Bp])
                    if oc < 2:
                        K.op("act", lambda e: e.activation(out=ckv[:, oc, :nn], in_=ps[:, :nn], func=AF.Copy), reads=[Bp], writes=[K.B("ckv")])
                    else:
                        K.op("act", lambda e: e.activation(out=krb[:, oc - 2, :nn], in_=ps[:, :nn], func=AF.Copy), reads=[Bp], writes=[K.B("krb")])
                kg = VEC["kvg"]
                norm_block(K, T, ckv[:, :, :nn], [K.B("ckv")], nn, lambda c: vecs[:, kg + c:kg + c + 1], None,
                           lambda c: (KL[:, c, t0:t0 + nn], K.B("KL", c, t0)), "kv")
                K.op("dve", lambda e: e.tensor_tensor(out=ra[:, :, :nn], in0=krb[:, :, :nn], in1=tb[:, :, :nn], op=ALU.mult),
                     reads=[K.B("krb"), K.B("tb")], writes=[K.B("ra")])
                K.op("dve", lambda e: e.tensor_tensor(out=KL[:, 2, t0:t0 + nn], in0=ra[:, 0, :nn], in1=ra[:, 1, :nn], op=ALU.add),
                     reads=[K.B("ra")], writes=[K.B("KL", 2, t0)])
            K.barrier()
        latb = [K.B("KL", c, t0) for c in range(3) for (t0, _) in LATBL]
        K.dma("sp", T.d_kxin.rearrange("(c p) t -> p c t", p=128), KL[:, :, CTX:TOK], reads=latb, writes=[K.B("kxin")])
        K.collective("AllGather", T.groups, T.d_kxin, T.d_kxout, reads=[K.B("kxin")], writes=[K.B("kxout")])
        kx = T.d_kxout.rearrange("(r c p) t -> r p c t", r=2, p=128)
        K.dma("sp", KL[:, :, CTX:TOK], kx[0], reads=[K.B("kxout")], writes=[K.B("KLr0")] + latb)
        K.dma("sp", KL[:, :, TOK:NKEY], kx[1], reads=[K.B("kxout")], writes=[K.B("KLr1")])
        KLb = [K.B("KLr0"), K.B("KLr1")] + [K.B("KL", c, 0) for c in range(3)]
        if T.dbg:
            K.dma("sp", T.dbg["KL"], KL[:], reads=KLb, writes=[K.B("dbgKL")])
        cqn = R("cqn", [128, 3, HALF], BF16)
        qrope = R("qrope", [128, 4, HALF], BF16)
        w_uq3 = T.d_full["mla_w_uq"].rearrange("(kc p) n -> p kc n", p=128)
        with ExitStack() as qes:
            S_ = lambda name, shape, dt: qes.enter_context(nc.sbuf_tensor(name, shape, dt, side="right"))
            wq = S_("wq", [128, 8, 384], BF16)
            K.dma("pool", wq[:], w_in3[:, :, 0:384], reads=[K.B("W", "mla_w_in")], writes=[K.B("wq")])
            wuqr = S_("wuqr", [128, 3, 1024], BF16)
            K.dma("pool", wuqr[:], w_uq3[:, :, 1024:2048], reads=[K.B("W", "mla_w_uq")], writes=[K.B("wuqr")])
            hb = S_("hbQ", [128, 8, 512], BF16)
            cq = S_("cqQ", [128, 3, 512], F32)
            tb = S_("tbQ", [128, 2, 512], F32)
            ra = [S_("raQ%d" % j, [128, 2, 512], F32) for j in range(2)]
            T.nrm = make_nrm(nc, qes, "Q", 1, 2, 2)
            rc = 0
            for (t0, nn) in LATBL:
                l0 = t0 - CTX
                gs = DV["gsmix1"]
                K.dma("sp", tb[:, :, :nn], T.d_cs[:, :, t0:t0 + nn], writes=[K.B("tbq")])
                norm_block(K, T, T.x_res[:, :, t0:t0 + nn], xb(t0), nn,
                           lambda c: dv[:, gs + c:gs + c + 1],
                           lambda c: dv[:, modcol(1, 0, c, 0):modcol(1, 0, c, 0) + 1],
                           lambda c: (hb[:, c, :nn], K.B("hbq", c)), "q")
                for oc in range(3):
                    ps, Bp = T.psum.next()
                    for kc in range(8):
                        mm(K, ps[:, :nn], wq[:, kc, oc * 128:(oc + 1) * 128], hb[:, kc, :nn], kc == 0, kc == 7,
                           reads=[K.B("wq"), K.B("hbq", kc)], writes=[Bp])
                    K.op("act", lambda e: e.activation(out=cq[:, oc, :nn], in_=ps[:, :nn], func=AF.Copy), reads=[Bp], writes=[K.B("cq")])
                qg = VEC["qg"]
                norm_block(K, T, cq[:, :, :nn], [K.B("cq")], nn, lambda c: vecs[:, qg + c:qg + c + 1], None,
                           lambda c: (cqn[:, c, l0:l0 + nn], K.B("cqn", c, l0)), "qn")
                for pr in range(4):
                    psa, Bpa = T.psum.next()
                    psb, Bpb = T.psum.next()
                    for (ps, Bp, off) in ((psa, Bpa, pr), (psb, Bpb, 4 + pr)):
                        for kc in range(3):
                            mm(K, ps[:, :nn], wuqr[:, kc, off * 128:(off + 1) * 128], cqn[:, kc, l0:l0 + nn], kc == 0, kc == 2,
                               reads=[K.B("wuqr"), K.B("cqn", kc, l0)], writes=[Bp])
                    r_, Br = ra[rc % 2], K.B("raq", rc % 2)
                    rc += 1
                    K.op("dve", lambda e: e.tensor_tensor(out=r_[:, 0, :nn], in0=psa[:, :nn], in1=tb[:, 0, :nn], op=ALU.mult),
                         reads=[Bpa, K.B("tbq")], writes=[Br])
                    K.op("dve", lambda e: e.tensor_tensor(out=r_[:, 1, :nn], in0=psb[:, :nn], in1=tb[:, 1, :nn], op=ALU.mult),
                         reads=[Bpb, K.B("tbq"), Br], writes=[Br])
                    K.op("dve", lambda e: e.tensor_tensor(out=qrope[:, pr, l0:l0 + nn], in0=r_[:, 0, :nn], in1=r_[:, 1, :nn], op=ALU.add),
                         reads=[Br], writes=[K.B("qrope", pr, l0)])
            K.barrier()
        if T.dbg:
            K.dma("sp", T.dbg["cqn"], cqn[:], reads=[K.B("cqn", c, t0 - CTX) for c in range(3) for (t0, _) in LATBL], writes=[K.B("dbgcqn")])
            K.dma("sp", T.dbg["qrope"], qrope[:], reads=[K.B("qrope", c, t0 - CTX) for c in range(4) for (t0, _) in LATBL], writes=[K.B("dbgqrope")])
        ring = PsumRing(K, T.banks, range(4))
        wukv = R("wukv", [128, 2, 2048], BF16)
        K.dma("pool", wukv[:], T.d_full["mla_w_ukv"].rearrange("(kc p) n -> p kc n", p=128), reads=[K.B("W", "mla_w_ukv")], writes=[K.B("wukv")])
        wuqn = R("wuqn", [128, 3, 1024], BF16)
        K.dma("pool", wuqn[:], w_uq3[:, :, 0:1024], reads=[K.B("W", "mla_w_uq")], writes=[K.B("wuqn")])
        Kh = R("Kh", [128, NKEY], BF16)
        Vh = R("Vh", [128, NKEY], BF16)
        qh = R("qh", [128, HALF], BF16)
        qz = R("qz", [128, HALF], BF16)
        woh = [R("woh%d" % j, [128, 1024], BF16) for j in range(2)]
        pt = [R("pt%d" % j, [128, 512], BF16) for j in range(4)]
        rd = [R("rd%d" % j, [128, 512], F32) for j in range(2)]
        oh_ = [R("oh%d" % j, [128, 512], BF16) for j in range(2)]
        ptc = 0
        obc = 0
        pending = None
        for h in range(NHEAD if not T.dbg else 2):
            K.dma("pool", woh[h % 2][:], T.d_full["mla_w_o"][h * 128:(h + 1) * 128, :], reads=[K.B("W", "mla_w_o")], writes=[K.B("woh", h % 2)])
            for (k0, kn) in KEYBL:
                ps, Bp = ring.next()
                for kc in range(2):
                    mm(K, ps[:, :kn], wukv[:, kc, h * 128:(h + 1) * 128], KL[:, kc, k0:k0 + kn], kc == 0, kc == 1,
                       reads=[K.B("wukv")] + KLb, writes=[Bp])
                K.op("act", lambda e: e.activation(out=Kh[:, k0:k0 + kn], in_=ps[:, :kn], func=AF.Copy), reads=[Bp], writes=[K.B("Kh")])
            for g0 in range(0, NTILE, 4):
                ng = min(4, NTILE - g0)
                ps, Bp = ring.next()
                for j in range(ng):
                    tl = g0 + j
                    for kc in range(2):
                        mm(K, ps[:, j * 128:(j + 1) * 128], KL[:, kc, tl * 128:(tl + 1) * 128],
                           wukv[:, kc, 1024 + h * 128:1024 + (h + 1) * 128], kc == 0, kc == 1,
                           reads=[K.B("wukv")] + KLb, writes=[Bp])
                K.op("dve", lambda e: e.tensor_copy(out=Vh[:, g0 * 128:(g0 + ng) * 128], in_=ps[:, :ng * 128]), reads=[Bp], writes=[K.B("Vh")])
            for (t0, nn) in LATBL:
                l0 = t0 - CTX
                ps, Bp = ring.next()
                for kc in range(3):
                    mm(K, ps[:, :nn], wuqn[:, kc, h * 128:(h + 1) * 128], cqn[:, kc, l0:l0 + nn], kc == 0, kc == 2,
                       reads=[K.B("wuqn"), K.B("cqn", kc, l0)], writes=[Bp])
                K.op("act", lambda e: e.activation(out=qh[:, l0:l0 + nn], in_=ps[:, :nn], func=AF.Copy), reads=[Bp], writes=[K.B("qh")])
            if T.dbg and h == 1:
                K.dma("sp", T.dbg["Kh"], Kh[:], reads=[K.B("Kh")], writes=[K.B("dbgKh")])
                K.dma("sp", T.dbg["Vh"], Vh[:], reads=[K.B("Vh")], writes=[K.B("dbgVh")])
                K.dma("sp", T.dbg["qh"], qh[:], reads=[K.B("qh")], writes=[K.B("dbgqh")])
            rb = (h % 2) * 64
            pr = h // 2
            ot = 64 - rb
            K.op("dve", lambda e: e.memset(qz[ot:ot + 64, :], 0.0), writes=[K.B("qz")])
            K.op("dve", lambda e: e.tensor_copy(out=qz[rb:rb + 64, :], in_=qrope[rb:rb + 64, pr, :]),
                 reads=[K.B("qrope", pr, t0 - CTX) for (t0, _) in LATBL], writes=[K.B("qz")])
            for (t0, nn) in LATBL:
                l0 = t0 - CTX
                ob = 4 + 2 * (obc % 2)
                obc += 1
                psO, BO = T.banks[ob], K.B("psum", ob)
                psD, BD = T.banks[ob + 1], K.B("psum", ob + 1)

                def S(tl, l0=l0, nn=nn):
                    ps, Bp = ring.next()
                    mm(K, ps[:, :nn], Kh[:, tl * 128:(tl + 1) * 128], qh[:, l0:l0 + nn], True, False,
                       reads=[K.B("Kh"), K.B("qh")], writes=[Bp])
                    mm(K, ps[:, :nn], KL[:, 2, tl * 128:(tl + 1) * 128], qz[:, l0:l0 + nn], False, True,
                       reads=KLb + [K.B("qz")], writes=[Bp])
                    return ps, Bp
                q_s = [S(0), S(1)]
                for tl in range(NTILE):
                    cur = q_s.pop(0)
                    if tl + 2 < NTILE:
                        q_s.append(S(tl + 2))
                    if tl == 3 and pending is not None:
                        pending()
                        pending = None
                    p_, Bpt = pt[ptc % 4], K.B("pt", ptc % 4)
                    ptc += 1
                    K.op("act", lambda e: e.activation(out=p_[:, :nn], in_=cur[0][:, :nn], func=AF.Exp, scale=SM_SCALE),
                         reads=[cur[1]], writes=[Bpt])
                    mm(K, psO[:, :nn], Vh[:, tl * 128:(tl + 1) * 128], p_[:, :nn], tl == 0, tl == NTILE - 1,
                       reads=[K.B("Vh"), Bpt], writes=[BO])
                    mm(K, psD[:, :nn], T.ones[:], p_[:, :nn], tl == 0, tl == NTILE - 1,
                       reads=[K.B("ones"), Bpt], writes=[BD])

                def finish(h=h, t0=t0, nn=nn, l0=l0, psO=psO, BO=BO, psD=psD, BD=BD, k=obc):
                    r_, Br = rd[k % 2], K.B("rd", k % 2)
                    o_, Bo = oh_[k % 2], K.B("ohb", k % 2)
                    K.op("dve", lambda e: e.reciprocal(out=r_[:, :nn], in_=psD[:, :nn]), reads=[BD], writes=[Br])
                    K.op("dve", lambda e: e.tensor_tensor(out=o_[:, :nn], in0=psO[:, :nn], in1=r_[:, :nn], op=ALU.mult),
                         reads=[BO, Br], writes=[Bo])
                    if T.dbg:
                        K.dma("sp", T.dbg["o"][:, h, l0:l0 + nn], o_[:, :nn], reads=[Bo], writes=[K.B("dbgo", h, l0)])
                    for oc in range(8):
                        ps, Bp = ring.next()
                        mm(K, ps[:, :nn], woh[h % 2][:, oc * 128:(oc + 1) * 128], o_[:, :nn], True, True,
                           reads=[K.B("woh", h % 2), Bo], writes=[Bp])
                        gc = modcol(1, 2, oc, 0)
                        Bxr = K.B("x", oc, t0)
                        K.op("dve", lambda e: e.scalar_tensor_tensor(out=T.x_res[:, oc, t0:t0 + nn], in0=ps[:, :nn], scalar=dv[:, gc:gc + 1],
                                                                     in1=T.x_res[:, oc, t0:t0 + nn], op0=ALU.mult, op1=ALU.add),
                             reads=[Bp, Bd, Bxr], writes=[Bxr])
                if (t0, nn) == LATBL[-1]:
                    finish()
                else:
                    pending = finish
        if pending is not None:
            pending()
        K.barrier()


def final_norm(K, nc, T):
    vecs = T.vecs
    with ExitStack() as les:
        R = lambda name, shape, dt: les.enter_context(nc.sbuf_tensor(name, shape, dt, side="right"))
        T.nrm = make_nrm(nc, les, "F", 2, 2, 2)
        ob = [R("ob%d" % j, [128, 8, 512], F32) for j in range(2)]
        fg = VEC["fg"]
        outs = []
        for bi, (t0, nn) in enumerate(LATBL):
            o_ = ob[bi % 2]
            Bo = [K.B("ob", bi % 2, c) for c in range(8)]
            norm_block(K, T, T.x_res[:, :, t0:t0 + nn], [K.B("x", c, t0) for c in range(8)], nn,
                       lambda c: vecs[:, fg + c:fg + c + 1], None,
                       lambda c: (o_[:, c, :nn], Bo[c]), "fin")
            Bout = K.B("out", bi)
            K.dma("sp", T.d_out[:, :, t0 - CTX:t0 - CTX + nn], o_[:, :, :nn], reads=Bo, writes=[Bout])
            outs.append(Bout)
        K.wait_all("sp", outs)


GROUPS = [[0, 1], [2, 3], [4, 5], [6, 7]]
FUSED = True


def build(stage):
    nc = bass.Bass("TRN2", target_bir_lowering=False)
    T = NS()
    T.groups = GROUPS

    def din(name, shape, dtype=F32):
        return nc.dram_tensor(name, shape, dtype, kind="ExternalInput").ap()

    def dout(name, shape, dtype=F32):
        return nc.dram_tensor(name, shape, dtype, kind="ExternalOutput").ap()

    T.d_vecs = din("vecs", [128, NV])
    T.d_adaw = [din("adaw0", [128, 6 * D]), din("adaw1", [128, 6 * D])]
    T.d_arin = nc.dram_tensor("arin", [128, 480], F32).ap()
    T.d_arout = nc.dram_tensor("arout", [128, 480], F32).ap()
    wlist = []
    if "A" in stage:
        wlist += [("lru_w_in", D, 2 * D), ("lru_w_out", D, D), ("ffn_w_in0", D, 2 * FFN_H), ("ffn_w_out0", FFN_H, D)]
    if "B" in stage:
        wlist += [("mla_w_in", D, 896), ("mla_w_uq", 384, 2048), ("mla_w_ukv", 256, 2048), ("mla_w_o", D, D),
                  ("ffn_w_in1", D, 2 * FFN_H), ("ffn_w_out1", FFN_H, D)]
    T.wnames = [w[0] for w in wlist]
    if stage == "AB":
        T.wearly, T.wlate = T.wnames[:4], T.wnames[4:]
    else:
        T.wearly, T.wlate = T.wnames, []
    T.d_sh, T.d_bounce, T.d_full = {}, {}, {}
    for (name, kk, nn) in wlist:
        T.d_sh[name] = din("sh_" + name, [kk // 8, nn])
        T.d_bounce[name] = nc.dram_tensor("bn_" + name, [kk // 8, nn], F32).ap()
        T.d_full[name] = nc.dram_tensor("full_" + name, [kk, nn], F32).ap()
    T.d_ffn_w_in = [T.d_full.get("ffn_w_in0"), T.d_full.get("ffn_w_in1")]
    T.d_ffn_w_out = [T.d_full.get("ffn_w_out0"), T.d_full.get("ffn_w_out1")]
    if "A" in stage:
        T.d_xT = din("xT", [128, 8, HALF + 2])
        T.d_ctxT = din("ctxT", [128, 8, CTX])
        T.d_lru_w_in = T.d_full["lru_w_in"]
        T.d_lru_gw = din("lru_gw", [4, 128, 2048])
        T.d_lru_w_out = T.d_full["lru_w_out"]
        T.d_cin = [nc.dram_tensor("cin%d" % n, [128, 2], F32).ap() for n in range(4)]
        T.d_cout = [nc.dram_tensor("cout%d" % n, [256, 2], F32).ap() for n in range(4)]
    if stage == "A":
        T.d_x1 = dout("x1T", [128, 8, HALF])
        T.d_ctx1 = dout("ctx1T", [128, 8, CTX])
    if "B" in stage:
        T.d_cs = din("cs", [128, 2, TOK])
        T.d_kxin = nc.dram_tensor("kxin", [384, HALF], BF16).ap()
        T.d_kxout = nc.dram_tensor("kxout", [768, HALF], BF16).ap()
        T.d_out = dout("outT", [128, 8, HALF])
    T.dbg = None
    if stage == "Bdbg":
        stage = "B"
        T.dbg = {"KL": dout("dbg_KL", [128, 3, NKEY], BF16), "cqn": dout("dbg_cqn", [128, 3, HALF], BF16),
                 "qrope": dout("dbg_qrope", [128, 4, HALF], BF16), "Kh": dout("dbg_Kh", [128, NKEY], BF16),
                 "Vh": dout("dbg_Vh", [128, NKEY], BF16), "qh": dout("dbg_qh", [128, HALF], BF16),
                 "o": dout("dbg_o", [128, 2, HALF], BF16), "x": dout("dbg_x", [128, 8, 512], F32)}
    if stage == "B":
        T.d_x1in = din("x1in", [128, 8, HALF])
        T.d_ctx1in = din("ctx1in", [128, 8, CTX])
    with ExitStack() as es:
        K = Kern(nc, es)
        T.banks = [es.enter_context(nc.psum_tensor("psb%d" % i, [128, 512], F32)) for i in range(8)]
        T.psum = PsumRing(K, T.banks, range(8))
        L = lambda name, shape, dt: es.enter_context(nc.sbuf_tensor(name, shape, dt, side="left"))
        T.vecs = L("vecs_sb", [128, NV], F32)
        T.dv = L("dv_sb", [128, ND], F32)
        T.ones = L("ones_sb", [128, 128], BF16)
        T.epsc = L("epsc_sb", [128, 1], F32)
        K.op("pool", lambda e: e.memset(T.epsc[:], EPS), writes=[K.B("epsc")])
        phase_prep(K, nc, T, es)
        K.barrier()
        if "A" in stage:
            yes = ExitStack()
            T.yg = yes.enter_context(nc.sbuf_tensor("yg_sb", [128, 8, TOK], BF16, side="right"))
            layer0_mixer(K, nc, T, es)
            K.barrier()
            T.x_res = L("x_res", [128, 8, TOK], F32)
            out_proj(K, nc, T, "lru_w_out", T.yg, 0, BL, True)
            yes.close()
            gather_weights(K, T, T.wlate)
            ffn(K, nc, T, 0, [BL[0:3], BL[3:5]])
        if stage == "B":
            T.x_res = L("x_res", [128, 8, TOK], F32)
            allx = [K.B("x", c, t0) for c in range(8) for (t0, _) in BL]
            K.dma("sp", T.x_res[:, :, CTX:TOK], T.d_x1in, writes=allx)
            K.dma("sp", T.x_res[:, :, 0:CTX], T.d_ctx1in, writes=[K.B("xctxload")] + [K.B("x", c, 0) for c in range(8)])
        if "B" in stage:
            layer1_mixer(K, nc, T, es)
            if T.dbg:
                K.dma("sp", T.dbg["x"], T.x_res[:, :, CTX:CTX + 512], reads=[K.B("x", c, CTX) for c in range(8)], writes=[K.B("dbgx")])
                K.barrier()
            else:
                ffn(K, nc, T, 1, [BL[1:3], BL[3:5]])
            final_norm(K, nc, T)
        if stage == "A":
            xb = [K.B("x", c, t0) for c in range(8) for (t0, _) in BL]
            K.dma("sp", T.d_x1, T.x_res[:, :, CTX:TOK], reads=xb, writes=[K.B("out_x1")])
            K.dma("sp", T.d_ctx1, T.x_res[:, :, 0:CTX], reads=xb, writes=[K.B("out_ctx1")])
            K.wait_all("sp", [K.B("out_x1"), K.B("out_ctx1")])
        print("instructions:", K.ninst, "sems:", K.nsem)
    return nc


def lat_index(half):
    j = np.arange(HALF + 2)
    return j if half == 0 else (SEQ - 1 - j)


def to_fm_tokens(a):
    nt = a.shape[0]
    return np.ascontiguousarray(a.T.reshape(8, 128, nt).transpose(1, 0, 2))


def from_fm_tokens(a):
    nt = a.shape[2]
    return a.transpose(1, 0, 2).reshape(1024, nt).T


def rows8(w, core):
    w = np.asarray(w, np.float32)
    k = w.shape[0] // 8
    return np.ascontiguousarray(w[core * k:(core + 1) * k])


def host_prep(inp, core):
    b, half = core // 2, core % 2
    f32 = np.float32
    m = {}
    vec = np.zeros((128, NV), f32)

    def put(name, arr):
        arr = np.asarray(arr, f32).reshape(128, -1)
        vec[:, VEC[name]:VEC[name] + arr.shape[1]] = arr
    ks = slice(core * 128, (core + 1) * 128)
    call = np.concatenate([np.asarray(inp["c"], f32), np.asarray(inp["c_ctx"], f32)[None]], 0)
    put("sc5", call[:, ks].T)
    oh = np.zeros((128, 4), f32)
    oh[:, b] = 1.0
    put("oh", oh)
    for i in range(2):
        put("gmix%d" % i, fm(inp["norm_mix_g"][i]))
        put("gffn%d" % i, fm(inp["norm_ffn_g"][i]))
        put("adab%d" % i, fm(inp["ada_b"][i]))
    cw = np.asarray(inp["lru_conv_w"][0], f32)
    z = np.zeros((1, D), f32)
    taps = np.concatenate([z, cw], 0) if half == 0 else np.concatenate([cw[::-1], z], 0)
    put("convw", np.moveaxis(fm(taps), 1, 2))
    put("convb", fm(inp["lru_conv_b"][0]))
    dirs = [0, 1] if half == 0 else [1, 0]
    gb = np.asarray(inp["lru_gate_b"][0], f32)[dirs]
    put("gateb", fm(gb))
    put("lam", fm(np.asarray(inp["lru_lambda"][0], f32)[dirs]))
    put("mask", np.tile(np.array([[0.0, 1.0]] if half == 0 else [[1.0, 0.0]], f32), (128, 1)))
    put("qg", fm(inp["mla_q_norm_g"][0]))
    put("kvg", fm(inp["mla_kv_norm_g"][0]))
    put("fg", fm(inp["final_norm_g"]))
    m["vecs"] = vec
    m["adaw0"] = np.ascontiguousarray(np.asarray(inp["ada_w"][0], f32)[ks])
    m["adaw1"] = np.ascontiguousarray(np.asarray(inp["ada_w"][1], f32)[ks])
    return m, dirs


def host_prep_A(inp, core):
    b, half = core // 2, core % 2
    m, dirs = host_prep(inp, core)
    f32 = np.float32
    m["xT"] = to_fm_tokens(np.asarray(inp["x"][b], f32)[lat_index(half)])
    ctx = np.asarray(inp["ctx"][b], f32)
    m["ctxT"] = to_fm_tokens(ctx if half == 0 else ctx[::-1])
    m["sh_lru_w_in"] = rows8(inp["lru_w_in"][0], core)
    m["sh_lru_w_out"] = rows8(inp["lru_w_out"][0], core)
    m["sh_ffn_w_in0"] = rows8(inp["ffn_w_in"][0], core)
    m["sh_ffn_w_out0"] = rows8(inp["ffn_w_out"][0], core)
    gw = np.asarray(inp["lru_gate_w"][0], f32)[dirs]
    gw = gw.reshape(2, 2, 4, 2, 128, 256).transpose(2, 4, 0, 1, 3, 5)
    m["lru_gw"] = np.ascontiguousarray(gw.reshape(4, 128, 2048))
    return m


def rope_tables(half):
    f32 = np.float32
    inv = (1.0 / (np.float32(10000.0) ** (np.arange(0, 32, 2, dtype=f32) / np.float32(32.0)))).astype(f32)
    tt = lat_index(half)[:HALF]
    pos = [(tt // 64).astype(f32), (tt % 64).astype(f32)]
    cs = np.zeros((128, 2, TOK), f32)
    cs[:, 0, :CTX] = 1.0
    for d in range(64):
        ang = (pos[d // 32] * inv[d % 16]).astype(f32)
        sign = -1.0 if (d % 32) < 16 else 1.0
        for r in (d, d + 64):
            cs[r, 0, CTX:] = np.cos(ang).astype(f32)
            cs[r, 1, CTX:] = sign * np.sin(ang).astype(f32)
    return cs


def rope_partner():
    d = np.arange(64)
    return np.where((d % 32) < 16, d + 16, d - 16)


def mla_layouts(inp):
    f32 = np.float32
    pp = rope_partner()
    w_in = np.asarray(inp["mla_w_in"][0], f32)
    kr = w_in[:, 640:704]
    w_in_h = np.concatenate([w_in[:, :640], kr, kr, kr[:, pp], kr[:, pp]], 1)
    w_uq = np.asarray(inp["mla_w_uq"][0], f32).reshape(384, 8, 192)
    nope = w_uq[:, :, :128].reshape(384, 1024)
    rope = w_uq[:, :, 128:]
    w_uq_h = np.concatenate([nope, rope.reshape(384, 512), rope[:, :, pp].reshape(384, 512)], 1)
    w_ukv = np.asarray(inp["mla_w_ukv"][0], f32).reshape(256, 8, 256)
    w_ukv_h = np.concatenate([w_ukv[:, :, :128].reshape(256, 1024), w_ukv[:, :, 128:].reshape(256, 1024)], 1)
    return w_in_h, w_uq_h, w_ukv_h


def host_prep_B(inp, core, lay, m=None):
    b, half = core // 2, core % 2
    if m is None:
        m, _ = host_prep(inp, core)
    w_in_h, w_uq_h, w_ukv_h = lay
    m["sh_mla_w_in"] = rows8(w_in_h, core)
    m["sh_mla_w_uq"] = rows8(w_uq_h, core)
    m["sh_mla_w_ukv"] = rows8(w_ukv_h, core)
    m["sh_mla_w_o"] = rows8(inp["mla_w_o"][0], core)
    m["sh_ffn_w_in1"] = rows8(inp["ffn_w_in"][1], core)
    m["sh_ffn_w_out1"] = rows8(inp["ffn_w_out"][1], core)
    m["cs"] = rope_tables(half)
    return m


_NC_CACHE = {}


def get_nc(stage):
    if stage not in _NC_CACHE:
        _NC_CACHE[stage] = build(stage)
    return _NC_CACHE[stage]


def assemble(results, key):
    out = np.zeros((4, SEQ, D), np.float32)
    for c in range(8):
        b, half = c // 2, c % 2
        out[b, lat_index(half)[:HALF]] = from_fm_tokens(results[c][key])
    return out


def kernel(**inputs):
    inp = {k: np.asarray(v) for k, v in inputs.items()}
    lay = mla_layouts(inp)
    if FUSED:
        maps = [host_prep_B(inp, c, lay, host_prep_A(inp, c)) for c in range(8)]
        res = run_bass_kernel_spmd(get_nc("AB"), maps, core_ids=list(range(8)))
        return assemble(res.results, "outT")
    mapsA = [host_prep_A(inp, c) for c in range(8)]
    resA = run_bass_kernel_spmd(get_nc("A"), mapsA, core_ids=list(range(8)))
    mapsB = []
    for c in range(8):
        m = host_prep_B(inp, c, lay)
        m["x1in"] = resA.results[c]["x1T"]
        m["ctx1in"] = resA.results[c]["ctx1T"]
        mapsB.append(m)
    resB = run_bass_kernel_spmd(get_nc("B"), mapsB, core_ids=list(range(8)))
    return assemble(resB.results, "outT")
```
